# Optimizing a Trainium2 kernel written in Bass

```python
import math
import jax, jax.numpy as jnp
from jax import lax
import numpy as np


D_MODEL = 1024
BATCH = 4
SEQ = 8192
DEPTH = 2

HEAD_DIM = 64
SB_HEADS = 8
SB_WIDTH = SB_HEADS * HEAD_DIM
SB_BLOCK = 128
MOBA_HEADS = 8
MOBA_WIDTH = MOBA_HEADS * HEAD_DIM
MOBA_BLOCK = 256
MOBA_TOPK = 3
MOBA_Q_CHUNK = 32
SWA_HEADS = 8
SWA_KV_HEADS = 2
SWA_GROUP = SWA_HEADS // SWA_KV_HEADS
SWA_WIDTH = SWA_HEADS * HEAD_DIM
SWA_KV_WIDTH = SWA_KV_HEADS * HEAD_DIM
SWA_WINDOW = 128
REL_BUCKETS = 32
REL_MAX_DIST = 128
REL_HEADS = MOBA_HEADS + SWA_HEADS
RMS_EPS = 1e-6

SPLIT_SIZES = (SB_WIDTH, SB_WIDTH, SB_WIDTH, SB_WIDTH,
               MOBA_WIDTH, MOBA_WIDTH, MOBA_WIDTH, MOBA_WIDTH,
               SWA_WIDTH, SWA_KV_WIDTH, SWA_KV_WIDTH, SWA_WIDTH,
               D_MODEL, D_MODEL, D_MODEL)
D_IN = sum(SPLIT_SIZES)

kernel_name = "hybrid_sb_moba_swa_gated_block"


def rms_norm(x, w):
    xf = x.astype(jnp.float32)
    y = xf * lax.rsqrt(jnp.mean(xf * xf, axis=-1, keepdims=True) + RMS_EPS)
    return (y * w.astype(jnp.float32)).astype(x.dtype)


def _split_columns(u):
    parts, off = [], 0
    for n in SPLIT_SIZES:
        parts.append(u[..., off:off + n])
        off += n
    return parts


def _rel_bucket(dist):
    max_exact = REL_BUCKETS // 2
    n = jnp.maximum(dist, 0)
    nf = jnp.maximum(n, 1).astype(jnp.float32)
    large = max_exact + (jnp.log(nf / max_exact) / math.log(REL_MAX_DIST / max_exact)
                         * (REL_BUCKETS - max_exact)).astype(jnp.int32)
    large = jnp.minimum(large, REL_BUCKETS - 1)
    return jnp.where(n < max_exact, n, large)


def stick_breaking_attention(q, k, v):
    B, H, S, dh = q.shape
    nq = S // SB_BLOCK
    scale = dh ** -0.5
    qb = q.reshape(B, H, nq, SB_BLOCK, dh).transpose(2, 0, 1, 3, 4)
    kpos = jnp.arange(S)

    def block(args):
        qi, i = args
        qpos = i * SB_BLOCK + jnp.arange(SB_BLOCK)
        z = jnp.einsum('bhqd,bhkd->bhqk', qi, k).astype(jnp.float32) * scale
        past = kpos[None, :] < qpos[:, None]
        log_1m = jnp.where(past, jax.nn.log_sigmoid(-z), 0.0)
        between = lax.cumsum(log_1m, axis=3, reverse=True) - log_1m
        w = jnp.where(past, jnp.exp(jax.nn.log_sigmoid(z) + between), 0.0)
        return jnp.einsum('bhqk,bhkd->bhqd', w.astype(v.dtype), v)

    out = lax.map(block, (qb, jnp.arange(nq)))
    return out.transpose(1, 2, 0, 3, 4).reshape(B, H, S, dh)


def moba_attention(q, k, v, rel_table):
    B, H, S, dh = q.shape
    scale = dh ** -0.5
    nb = -(-S // MOBA_BLOCK)
    pad = nb * MOBA_BLOCK - S
    padding = ((0, 0), (0, 0), (0, pad), (0, 0))
    kb = jnp.pad(k, padding).reshape(B, H, nb, MOBA_BLOCK, dh)
    vb = jnp.pad(v, padding).reshape(B, H, nb, MOBA_BLOCK, dh)
    kmean = kb.astype(jnp.float32).mean(axis=3).astype(k.dtype)

    pos = jnp.arange(S)
    own = pos // MOBA_BLOCK
    gate = jnp.einsum('bhsd,bhnd->bhsn', q, kmean).astype(jnp.float32)
    past_blk = jnp.arange(nb)[None, :] < own[:, None]
    gate = jnp.where(past_blk, gate, -jnp.inf)
    k_sel = min(MOBA_TOPK, nb)
    _, idx = lax.top_k(gate, k_sel)
    valid = idx < own[:, None]

    nc = S // MOBA_Q_CHUNK
    C = MOBA_Q_CHUNK
    qc = q.reshape(B, H, nc, C, dh).transpose(2, 0, 1, 3, 4)
    idxc = idx.reshape(B, H, nc, C, k_sel).transpose(2, 0, 1, 3, 4)
    validc = valid.reshape(B, H, nc, C, k_sel).transpose(2, 0, 1, 3, 4)
    table = rel_table.T
    bi = jnp.arange(B)[:, None, None, None]
    hi = jnp.arange(H)[None, :, None, None]
    offs = jnp.arange(MOBA_BLOCK)
    n_sel = k_sel * MOBA_BLOCK

    def chunk(args):
        qi, ii, vi, c = args
        start = c * C
        qpos = start + jnp.arange(C)
        ob = start // MOBA_BLOCK
        ks = kb[bi, hi, ii].reshape(B, H, C, n_sel, dh)
        vs = vb[bi, hi, ii].reshape(B, H, C, n_sel, dh)
        kpos_sel = (ii[..., None] * MOBA_BLOCK + offs).reshape(B, H, C, n_sel)
        s_sel = jnp.einsum('bhqd,bhqkd->bhqk', qi, ks).astype(jnp.float32) * scale
        s_sel = s_sel + table[hi, _rel_bucket(qpos[:, None] - kpos_sel)]
        s_sel = jnp.where(jnp.repeat(vi, MOBA_BLOCK, axis=-1), s_sel, -jnp.inf)
        ko = lax.dynamic_index_in_dim(kb, ob, axis=2, keepdims=False)
        vo = lax.dynamic_index_in_dim(vb, ob, axis=2, keepdims=False)
        d_own = qpos[:, None] - (ob * MOBA_BLOCK + offs)[None, :]
        s_own = jnp.einsum('bhqd,bhkd->bhqk', qi, ko).astype(jnp.float32) * scale
        s_own = jnp.where(d_own >= 0, s_own + table[:, _rel_bucket(d_own)], -jnp.inf)
        p = jax.nn.softmax(jnp.concatenate([s_sel, s_own], axis=-1), axis=-1).astype(v.dtype)
        return (jnp.einsum('bhqk,bhqkd->bhqd', p[..., :n_sel], vs)
                + jnp.einsum('bhqk,bhkd->bhqd', p[..., n_sel:], vo))

    out = lax.map(chunk, (qc, idxc, validc, jnp.arange(nc)))
    return out.transpose(1, 2, 0, 3, 4).reshape(B, H, S, dh)


def swa_attention(q, k, v, sinks, rel_table):
    B, S, Hq, dh = q.shape
    W = SWA_WINDOW
    nb = S // W
    scale = dh ** -0.5
    qb = q.reshape(B, nb, W, SWA_KV_HEADS, SWA_GROUP, dh)
    kb = k.reshape(B, nb, W, SWA_KV_HEADS, dh)
    vb = v.reshape(B, nb, W, SWA_KV_HEADS, dh)
    prev = ((0, 0), (1, 0), (0, 0), (0, 0), (0, 0))
    kw = jnp.concatenate([jnp.pad(kb, prev)[:, :-1], kb], axis=2)
    vw = jnp.concatenate([jnp.pad(vb, prev)[:, :-1], vb], axis=2)
    s = jnp.einsum('bnqhgd,bnchd->bnhgqc', qb, kw).astype(jnp.float32) * scale
    ci = jnp.arange(2 * W)[None, :]
    dist = jnp.arange(W)[:, None] + W - ci
    bias = rel_table[_rel_bucket(dist)].astype(jnp.float32)
    bias = bias.transpose(2, 0, 1).reshape(SWA_KV_HEADS, SWA_GROUP, W, 2 * W)
    in_band = (dist >= 0) & (dist < W)
    key_exists = (jnp.arange(nb)[:, None] * W - W + ci) >= 0
    mask = in_band[None] & key_exists[:, None, :]
    s = jnp.where(mask[None, :, None, None], s + bias, -jnp.inf)
    sink = sinks.astype(jnp.float32).reshape(SWA_KV_HEADS, SWA_GROUP)[None, None, :, :, None, None]
    m = jnp.maximum(s.max(axis=-1, keepdims=True), sink)
    e = jnp.exp(s - m)
    p = (e / (e.sum(axis=-1, keepdims=True) + jnp.exp(sink - m))).astype(v.dtype)
    out = jnp.einsum('bnhgqc,bnchd->bnqhgd', p, vw)
    return out.reshape(B, S, Hq, dh)


def hybrid_layer(x, norm_w, w_in, w_proj_a, w_proj_b, w_proj_c, w_out, sinks, rel_bias):
    B, S, _ = x.shape
    h = rms_norm(x, norm_w)
    u = jnp.einsum('bsd,de->bse', h, w_in)
    (qa, ka, va, ga, qb, kb, vb, gb, qc, kc, vc, gc, ma, mb, mc) = _split_columns(u)

    def bhsd(t, n):
        return t.reshape(B, S, n, HEAD_DIM).transpose(0, 2, 1, 3)

    ya = stick_breaking_attention(bhsd(qa, SB_HEADS), bhsd(ka, SB_HEADS), bhsd(va, SB_HEADS))
    ya = ya.transpose(0, 2, 1, 3).reshape(B, S, SB_WIDTH)
    yb = moba_attention(bhsd(qb, MOBA_HEADS), bhsd(kb, MOBA_HEADS), bhsd(vb, MOBA_HEADS),
                        rel_bias[:, :MOBA_HEADS])
    yb = yb.transpose(0, 2, 1, 3).reshape(B, S, MOBA_WIDTH)
    yc = swa_attention(qc.reshape(B, S, SWA_HEADS, HEAD_DIM),
                       kc.reshape(B, S, SWA_KV_HEADS, HEAD_DIM),
                       vc.reshape(B, S, SWA_KV_HEADS, HEAD_DIM),
                       sinks, rel_bias[:, MOBA_HEADS:])
    yc = yc.reshape(B, S, SWA_WIDTH)

    ya = jnp.einsum('bse,ed->bsd', ya * jax.nn.silu(ga), w_proj_a)
    yb = jnp.einsum('bse,ed->bsd', yb * jax.nn.silu(gb), w_proj_b)
    yc = jnp.einsum('bse,ed->bsd', yc * jax.nn.silu(gc), w_proj_c)
    merged = jax.nn.sigmoid(ma) * ya + jax.nn.sigmoid(mb) * yb + jax.nn.sigmoid(mc) * yc
    return x + jnp.einsum('bsd,de->bse', merged, w_out)


def setup_inputs(seed: int = 0) -> dict:
    key = jax.random.key(seed)
    ks = jax.random.split(key, 10)
    f32 = jnp.float32
    x = jax.random.normal(ks[0], (BATCH, SEQ, D_MODEL), f32)
    norm_w = 1.0 + 0.02 * jax.random.normal(ks[1], (DEPTH, D_MODEL), f32)
    w_in = jax.random.normal(ks[2], (DEPTH, D_MODEL, D_IN), f32) * D_MODEL ** -0.5
    w_proj_a = jax.random.normal(ks[3], (DEPTH, SB_WIDTH, D_MODEL), f32) * SB_WIDTH ** -0.5
    w_proj_b = jax.random.normal(ks[4], (DEPTH, MOBA_WIDTH, D_MODEL), f32) * MOBA_WIDTH ** -0.5
    w_proj_c = jax.random.normal(ks[5], (DEPTH, SWA_WIDTH, D_MODEL), f32) * SWA_WIDTH ** -0.5
    w_out = jax.random.normal(ks[6], (DEPTH, D_MODEL, D_MODEL), f32) * D_MODEL ** -0.5
    sinks = 0.5 * jax.random.normal(ks[7], (DEPTH, SWA_HEADS), f32)
    rel_bias = 0.5 * jax.random.normal(ks[8], (REL_BUCKETS, REL_HEADS), f32)
    final_norm_w = 1.0 + 0.02 * jax.random.normal(ks[9], (D_MODEL,), f32)
    return {"x": x, "norm_w": norm_w, "w_in": w_in, "w_proj_a": w_proj_a,
            "w_proj_b": w_proj_b, "w_proj_c": w_proj_c, "w_out": w_out,
            "sinks": sinks, "rel_bias": rel_bias, "final_norm_w": final_norm_w}


def reference(x, norm_w, w_in, w_proj_a, w_proj_b, w_proj_c, w_out, sinks, rel_bias, final_norm_w):
    for layer in range(DEPTH):
        x = hybrid_layer(x, norm_w[layer], w_in[layer], w_proj_a[layer], w_proj_b[layer],
                         w_proj_c[layer], w_out[layer], sinks[layer], rel_bias)
    return rms_norm(x, final_norm_w)
```

```python
import numpy as np
from contextlib import ExitStack
import ml_dtypes
import concourse.bass as bass
import concourse.mybir as mybir
from concourse.bass_utils import run_bass_kernel_spmd


ENGINES = ("pe", "act", "dve", "pool", "sp")
N_DMA_SEMS = 32
N_CC_SEMS = 24


class Sems:
    def __init__(self, nc, stack):
        self.nc = nc
        self.stack = stack
        self.gen = 0
        self.cnt = {e: stack.enter_context(nc.semaphore("cnt_" + e)) for e in ENGINES}
        self.cnt_val = {e: 0 for e in ENGINES}
        self.dma = [stack.enter_context(nc.semaphore("dma%d" % i)) for i in range(N_DMA_SEMS)]
        self.dma_val = [0] * N_DMA_SEMS
        self.dma_next = 0
        self.cc = [stack.enter_context(nc.semaphore("cc%d" % i)) for i in range(N_CC_SEMS)]
        self.cc_next = 0
        self.ext = {}
        self.pending = []


def fresh_counters(sems):
    sems.gen += 1
    sems.cnt = {e: sems.stack.enter_context(sems.nc.semaphore("cnt%d_%s" % (sems.gen, e))) for e in ENGINES}
    sems.cnt_val = {e: 0 for e in ENGINES}


class Phase:
    def __init__(self, nc, sems):
        self.nc = nc
        self.sems = sems
        self.ops = []
        self.per_eng = {e: [] for e in ENGINES}
        for (fn, res) in sems.pending:
            self.cc(fn, res)
        sems.pending = []

    def op(self, eng, fn, reads=(), writes=(), dma=False, ext=(), cc=None):
        i = len(self.ops)
        self.ops.append(dict(eng=eng, fn=fn, reads=tuple(reads), writes=tuple(writes),
                             dma=dma, deps=set(), marked=False, ext=tuple(ext), cc=cc))
        self.per_eng[eng].append(i)
        return i

    def cc(self, fn, res):
        sems = self.sems
        sem = sems.cc[sems.cc_next]
        sems.cc_next += 1
        i = self.op("pool", fn, ext=res, cc=sem)
        for r in res:
            sems.ext[r] = (sem, 1)
        return i

    def pe(self, fn, reads=(), writes=()):
        return self.op("pe", fn, reads, writes)

    def act(self, fn, reads=(), writes=()):
        return self.op("act", fn, reads, writes)

    def dve(self, fn, reads=(), writes=()):
        return self.op("dve", fn, reads, writes)

    def pool(self, fn, reads=(), writes=()):
        return self.op("pool", fn, reads, writes)

    def dma(self, q, fn, reads=(), writes=(), ext=()):
        return self.op(q, fn, reads, writes, dma=True, ext=ext)

    def emit(self):
        nc, sems, ops = self.nc, self.sems, self.ops
        last_w = {}
        readers = {}
        for i, o in enumerate(ops):
            deps = set()
            for r in o["reads"]:
                if r in last_w:
                    deps.add(last_w[r])
                if isinstance(r, tuple) and r and r[0] == "ps":
                    for j in readers.get(r, ()):
                        if ops[j]["eng"] != o["eng"]:
                            deps.add(j)
            for w in o["writes"]:
                if w in last_w:
                    deps.add(last_w[w])
                for j in readers.get(w, ()):
                    deps.add(j)
            for r in o["reads"]:
                readers.setdefault(r, []).append(i)
            for w in o["writes"]:
                last_w[w] = i
                readers[w] = []
            keep = set()
            for j in deps:
                pj = ops[j]
                if pj["dma"]:
                    keep.add(j)
                    continue
                if pj["eng"] == o["eng"] and not o["dma"]:
                    if o["eng"] == "pe":
                        continue
                    raw = any(w in o["reads"] for w in pj["writes"])
                    if not raw:
                        continue
                keep.add(j)
            o["deps"] = keep
            for j in keep:
                ops[j]["marked"] = True
        for i, o in enumerate(ops):
            if o["dma"]:
                k = sems.dma_next
                sems.dma_next += 1
                s = k % N_DMA_SEMS
                o["dsem"] = s
                o["dprev"] = sems.dma_val[s]
                sems.dma_val[s] += 16
                o["dval"] = sems.dma_val[s]
        for e in ENGINES:
            for i in self.per_eng[e]:
                o = ops[i]
                if (not o["dma"]) and o["marked"]:
                    sems.cnt_val[e] += 1
                    o["cval"] = sems.cnt_val[e]
        final_dma = dict((s, sems.dma_val[s]) for s in range(N_DMA_SEMS))
        ext_snapshot = {}
        for i, o in enumerate(ops):
            if o["ext"]:
                if o["cc"] is not None:
                    ext_snapshot[i] = dict((r, sv) for r, sv in sems.ext.items() if sv[0] is not o["cc"])
                else:
                    ext_snapshot[i] = dict(sems.ext)

        def body(e):
            def f(eng):
                known_cnt = {}
                known_dma = {}
                known_ext = {}
                for i in self.per_eng[e]:
                    o = ops[i]
                    need_cnt = {}
                    need_dma = {}
                    for j in o["deps"]:
                        pj = ops[j]
                        if pj["dma"]:
                            s = pj["dsem"]
                            need_dma[s] = max(need_dma.get(s, 0), pj["dval"])
                        else:
                            need_cnt[pj["eng"]] = max(need_cnt.get(pj["eng"], 0), pj["cval"])
                    if o["dma"] and o["dprev"] > 0:
                        s = o["dsem"]
                        need_dma[s] = max(need_dma.get(s, 0), o["dprev"])
                    for r in o["ext"]:
                        if r in ext_snapshot[i]:
                            sem, v = ext_snapshot[i][r]
                            key = id(sem)
                            if known_ext.get(key, 0) < v:
                                eng.wait_ge(sem, v)
                                known_ext[key] = v
                    for fe, v in need_cnt.items():
                        if known_cnt.get(fe, 0) < v:
                            eng.wait_ge(sems.cnt[fe], v)
                            known_cnt[fe] = v
                    for s, v in need_dma.items():
                        if known_dma.get(s, 0) < v:
                            eng.wait_ge(sems.dma[s], v)
                            known_dma[s] = v
                    ins = o["fn"](eng)
                    if o["cc"] is not None:
                        ins.then_inc(o["cc"], 1)
                    elif o["dma"]:
                        ins.then_inc(sems.dma[o["dsem"]], 16)
                    elif o["marked"]:
                        ins.then_inc(sems.cnt[e], 1)
                if e == "sp":
                    for s in range(N_DMA_SEMS):
                        if final_dma[s] > 0 and known_dma.get(s, 0) < final_dma[s]:
                            eng.wait_ge(sems.dma[s], final_dma[s])
            return f

        with nc.Block() as block:
            block.tensor(body("pe"))
            block.scalar(body("act"))
            block.vector(body("dve"))
            block.gpsimd(body("pool"))
            block.sync(body("sp"))


F32 = mybir.dt.float32
BF16 = mybir.dt.bfloat16
AF = mybir.ActivationFunctionType
ALU = mybir.AluOpType
AX = mybir.AxisListType

D = 1024
NCH = D // 128
EPS = 1e-6
R_QA, R_KA, R_GA, R_QB, R_KB, R_GB, R_QC, R_GC, R_KC = 0, 256, 512, 768, 1024, 1280, 1536, 1792, 2048
NFEAT = 2112
C_VA, C_VB, C_VC = 0, 256, 512
NV = 576
NEG = -30000.0


def ps_res(b):
    return ("ps", b)


_uid = [0]


def U(name):
    _uid[0] += 1
    return "%s_%d" % (name, _uid[0])


def rmsnorm_tile(ph, nc, xt, hb, sq, ss, rs, normw_bc, tag, slot, d_in=D):
    ph.act(lambda e: e.activation(out=sq[:], in_=xt[:], func=AF.Square, accum_out=ss[:, 0:1]),
           reads=[("x", tag, slot)], writes=[("sq", tag), ("ss", tag, slot)])
    ph.dve(lambda e: e.tensor_scalar(out=rs[:, 0:1], in0=ss[:, 0:1], scalar1=1.0 / d_in, scalar2=EPS,
                                     op0=ALU.mult, op1=ALU.add),
           reads=[("ss", tag, slot)], writes=[("rs", tag, slot)])
    ph.act(lambda e: e.activation(out=rs[:, 1:2], in_=rs[:, 0:1], func=AF.Sqrt),
           reads=[("rs", tag, slot)], writes=[("rs1", tag, slot)])
    ph.dve(lambda e: e.reciprocal(out=rs[:, 2:3], in_=rs[:, 1:2]),
           reads=[("rs1", tag, slot)], writes=[("rs2", tag, slot)])
    ph.dve(lambda e: e.scalar_tensor_tensor(out=hb[:], in0=xt[:], scalar=rs[:, 2:3], in1=normw_bc[:],
                                            op0=ALU.mult, op1=ALU.mult),
           reads=[("x", tag, slot), ("rs2", tag, slot), "normw"], writes=[("hb", tag, slot)])


def phase_front(nc, sems, st, S, xsrc, normw, w, featT, vtok, kmean, ident):
    NW = NFEAT + NV
    ph = Phase(nc, sems)
    with ExitStack() as ls:
        wsb = ls.enter_context(nc.sbuf_tensor(U("wsb"), [128, NCH, NW], BF16))
        nwb = ls.enter_context(nc.sbuf_tensor(U("nwb"), [128, D], F32))
        xts = [ls.enter_context(nc.sbuf_tensor(U("xt%d" % i), [128, D], F32)) for i in range(2)]
        sq = ls.enter_context(nc.sbuf_tensor(U("sq"), [128, D], BF16))
        sss = [ls.enter_context(nc.sbuf_tensor(U("ss%d" % i), [128, 1], F32)) for i in range(2)]
        rss = [ls.enter_context(nc.sbuf_tensor(U("rs%d" % i), [128, 4], F32)) for i in range(2)]
        hbs = [ls.enter_context(nc.sbuf_tensor(U("hb%d" % i), [128, D], BF16)) for i in range(2)]
        hTs = [ls.enter_context(nc.sbuf_tensor(U("hT%d" % i), [128, NCH, 512], BF16)) for i in range(2)]
        fst = [ls.enter_context(nc.sbuf_tensor(U("fst%d" % i), [128, 17, 512], BF16)) for i in range(2)]
        vst = [ls.enter_context(nc.sbuf_tensor(U("vst%d" % i), [128, 4, NV], BF16)) for i in range(2)]
        kacc = ls.enter_context(nc.sbuf_tensor(U("kacc"), [128, 2, 64], F32))
        ptr = [ls.enter_context(nc.psum_tensor(U("ptr%d" % i), [128, D], BF16)) for i in range(2)]
        pmm = [ls.enter_context(nc.psum_tensor(U("pmm%d" % i), [128, 512], F32)) for i in range(4)]

        for c in range(NCH):
            for hh in range(2):
                c0, c1 = hh * (NW // 2), (hh + 1) * (NW // 2)
                ph.dma("pool", lambda e, c=c, c0=c0, c1=c1: e.dma_start(
                    out=wsb[:, c, c0:c1], in_=w[c * 128:(c + 1) * 128, c0:c1]), writes=[("w", c, hh)])
        ph.dma("sp", lambda e: e.dma_start(out=nwb[:], in_=normw.partition_broadcast(128)), writes=["normw"])
        wres = [("w", c, hh) for c in range(NCH) for hh in range(2)]

        nblk = S // 512
        tt = 0
        mmi = 0
        for b in range(nblk):
            hT = hTs[b % 2]
            for ti in range(4):
                sl = tt % 2
                tt += 1
                r0 = b * 512 + ti * 128
                xap, xe = xsrc(r0)
                ph.dma("sp", lambda e, sl=sl, xap=xap: e.dma_start(out=xts[sl][:], in_=xap),
                       writes=[("x", "f", sl)], ext=xe)
                rmsnorm_tile(ph, nc, xts[sl], hbs[sl], sq, sss[sl], rss[sl], nwb, "f", sl)
                pt = ptr[sl]
                for c in range(NCH):
                    ph.pe(lambda e, c=c, sl=sl, pt=pt: e.transpose(out=pt[:, c * 128:(c + 1) * 128],
                                                                  in_=hbs[sl][:, c * 128:(c + 1) * 128],
                                                                  identity=ident[:]),
                          reads=[("hb", "f", sl), "ident"], writes=[ps_res(("tr", sl))])
                ph.dve(lambda e, pt=pt, hT=hT, ti=ti: e.tensor_copy(
                    out=hT[:, :, ti * 128:(ti + 1) * 128],
                    in_=pt[:].rearrange("p (c t) -> p c t", c=NCH)),
                    reads=[ps_res(("tr", sl))], writes=[("hT", b % 2, ti)])
            hres = [("hT", b % 2, ti) for ti in range(4)]
            fs = fst[b % 2]
            for g in range(17):
                rows = 128 if g < 16 else 64
                pm = pmm[mmi % 4]
                pres = ps_res(("mm", mmi % 4))
                mmi += 1
                for c in range(NCH):
                    ph.pe(lambda e, c=c, g=g, rows=rows, pm=pm, hT=hT: e.matmul(
                        pm[0:rows, :], wsb[:, c, g * 128:g * 128 + rows], hT[:, c, :],
                        start=(c == 0), stop=(c == NCH - 1)),
                        reads=wres + hres, writes=[pres])
                r = g * 128
                isq = (r in (R_QA, R_QA + 128, R_QB, R_QB + 128, R_QC, R_QC + 128))
                iskb = (r in (R_KB, R_KB + 128))
                if iskb:
                    j = (r - R_KB) // 128
                    for half in range(2):
                        ph.act(lambda e, pm=pm, fs=fs, g=g, half=half, j=j, b=b: e.activation(
                            out=fs[:, g, half * 256:(half + 1) * 256], in_=pm[:, half * 256:(half + 1) * 256],
                            func=AF.Copy, accum_out=kacc[:, j, 2 * b + half:2 * b + half + 1]),
                            reads=[pres], writes=[("fs", b % 2, g, half), "kacc"])
                elif g % 2 == 0:
                    ph.act(lambda e, pm=pm, fs=fs, g=g, rows=rows, isq=isq: e.activation(
                        out=fs[0:rows, g, :], in_=pm[0:rows, :], func=AF.Copy, scale=(0.125 if isq else 1.0)),
                        reads=[pres], writes=[("fs", b % 2, g)])
                else:
                    ph.dve(lambda e, pm=pm, fs=fs, g=g, rows=rows, isq=isq: e.tensor_scalar(
                        out=fs[0:rows, g, :], in0=pm[0:rows, :], scalar1=(0.125 if isq else 1.0), scalar2=None,
                        op0=ALU.mult),
                        reads=[pres], writes=[("fs", b % 2, g)])
            fsres = [("fs", b % 2, g) for g in range(17)] + [("fs", b % 2, g, h) for g in (8, 9) for h in range(2)]
            ph.dma("sp", lambda e, fs=fs, b=b: e.dma_start(
                out=featT[0:2048, b * 512:(b + 1) * 512].rearrange("(g p) s -> p g s", p=128),
                in_=fs[:, 0:16, :]), reads=fsres, writes=[("featT", b)])
            ph.dma("sp", lambda e, fs=fs, b=b: e.dma_start(
                out=featT[2048:2112, b * 512:(b + 1) * 512], in_=fs[0:64, 16, :]),
                reads=fsres, writes=[("featT2", b)])
            vs = vst[b % 2]
            for ti in range(4):
                for (c0, c1) in ((0, 512), (512, NV)):
                    pm = pmm[mmi % 4]
                    pres = ps_res(("mm", mmi % 4))
                    mmi += 1
                    for c in range(NCH):
                        ph.pe(lambda e, c=c, c0=c0, c1=c1, pm=pm, hT=hT, ti=ti: e.matmul(
                            pm[:, 0:c1 - c0], hT[:, c, ti * 128:(ti + 1) * 128], wsb[:, c, NFEAT + c0:NFEAT + c1],
                            start=(c == 0), stop=(c == NCH - 1)),
                            reads=wres + hres, writes=[pres])
                    if c0 == 0:
                        ph.dve(lambda e, pm=pm, vs=vs, ti=ti, c0=c0, c1=c1: e.tensor_copy(
                            out=vs[:, ti, c0:c1], in_=pm[:, 0:c1 - c0]),
                            reads=[pres], writes=[("vs", b % 2, ti, c0)])
                    else:
                        ph.act(lambda e, pm=pm, vs=vs, ti=ti, c0=c0, c1=c1: e.activation(
                            out=vs[:, ti, c0:c1], in_=pm[:, 0:c1 - c0], func=AF.Copy),
                            reads=[pres], writes=[("vs", b % 2, ti, c0)])
            vsres = [("vs", b % 2, ti, c0) for ti in range(4) for c0 in (0, 512)]
            ph.dma("act", lambda e, vs=vs, b=b: e.dma_start(
                out=vtok[b * 512:(b + 1) * 512, :].rearrange("(t p) c -> p t c", p=128), in_=vs[:]),
                reads=vsres, writes=[("vtok", b)])
        ph.dve(lambda e: e.tensor_scalar(out=kmean[:], in0=kacc[:, :, 0:S // 256], scalar1=1.0 / 256.0, scalar2=None,
                                         op0=ALU.mult), reads=["kacc"], writes=["kmean"])
        ph.emit()


def phase_sb(nc, sems, st, S, featT, vtok, yg, trineg, onesneg, sbmask):
    NQT = S // 512
    for hp in range(2):
        ph = Phase(nc, sems)
        with ExitStack() as ls:
            QT = ls.enter_context(nc.sbuf_tensor(U("sbQT"), [128, S], BF16))
            KT = ls.enter_context(nc.sbuf_tensor(U("sbKT"), [128, S], BF16))
            GS = ls.enter_context(nc.sbuf_tensor(U("sbGS"), [128, S], BF16))
            Vt = ls.enter_context(nc.sbuf_tensor(U("sbV"), [128, S // 128, 128], BF16))
            Es = [ls.enter_context(nc.sbuf_tensor(U("sbE%d" % i), [128, 512], F32)) for i in range(2)]
            SPs = [ls.enter_context(nc.sbuf_tensor(U("sbSP%d" % i), [128, 512], BF16)) for i in range(2)]
            Bss = [ls.enter_context(nc.sbuf_tensor(U("sbBs%d" % i), [128, 512], F32)) for i in range(2)]
            Ws = [ls.enter_context(nc.sbuf_tensor(U("sbW%d" % i), [128, 512], BF16)) for i in range(2)]
            Cc = ls.enter_context(nc.sbuf_tensor(U("sbC"), [128, 512], F32))
            ys = [ls.enter_context(nc.sbuf_tensor(U("sbys%d" % i), [64, 512], BF16)) for i in range(2)]
            pz = [ls.enter_context(nc.psum_tensor(U("pz%d" % i), [128, 512], F32)) for i in range(2)]
            pb = [ls.enter_context(nc.psum_tensor(U("pb%d" % i), [128, 512], F32)) for i in range(2)]
            pd = [ls.enter_context(nc.psum_tensor(U("pd%d" % i), [128, 512], F32)) for i in range(2)]
            po = [ls.enter_context(nc.psum_tensor(U("po%d" % i), [64, 512], F32)) for i in range(2)]

            ph.dma("sp", lambda e: e.dma_start(out=QT[:], in_=featT[R_QA + hp * 128:R_QA + (hp + 1) * 128, :]), writes=["QT"])
            ph.dma("act", lambda e: e.dma_start(out=KT[:], in_=featT[R_KA + hp * 128:R_KA + (hp + 1) * 128, :]), writes=["KT"])
            ph.dma("sp", lambda e: e.dma_start(out=GS[:], in_=featT[R_GA + hp * 128:R_GA + (hp + 1) * 128, :]), writes=["GS"])
            ph.dma("act", lambda e: e.dma_start(
                out=Vt[:], in_=vtok[:, C_VA + hp * 128:C_VA + (hp + 1) * 128].rearrange("(t p) c -> p t c", p=128)),
                writes=["Vt"])
            for c0 in range(0, S, 2048):
                c1 = min(S, c0 + 2048)
                ph.act(lambda e, c0=c0, c1=c1: e.activation(out=GS[:, c0:c1], in_=GS[:, c0:c1], func=AF.Silu),
                       reads=["GS"], writes=["GS"])

            steps = []
            hq = 0
            for hl in range(2):
                for qt in range(NQT):
                    nkb = 4 * (qt + 1)
                    for t in range(nkb):
                        kb = nkb - 1 - t
                        steps.append(dict(hl=hl, qt=qt, t=t, kb=kb, nkb=nkb, hq=hq,
                                          diag=(kb - 4 * qt) if kb >= 4 * qt else None))
                    hq += 1
            n = len(steps)

            def stageA(s):
                d = steps[s]
                hb = d["hl"] * 64
                sl = s % 2
                kb, qt = d["kb"], d["qt"]
                ph.pe(lambda e: e.matmul(pz[sl][:], KT[hb:hb + 64, kb * 128:(kb + 1) * 128],
                                         QT[hb:hb + 64, qt * 512:(qt + 1) * 512], start=True, stop=True),
                      reads=["QT", "KT"], writes=[ps_res(("z", sl))])
                ph.act(lambda e: e.activation(out=Es[sl][:], in_=pz[sl][:], func=AF.Exp),
                       reads=[ps_res(("z", sl))], writes=[("E", sl)])
                ph.act(lambda e: e.activation(out=SPs[sl][:], in_=Es[sl][:], func=AF.Ln, bias=1.0),
                       reads=[("E", sl)], writes=[("SP", sl)])
                if d["diag"] is not None:
                    i = d["diag"]
                    ph.pool(lambda e: e.tensor_tensor(out=SPs[sl][:], in0=SPs[sl][:], in1=sbmask[:, i, :], op=ALU.mult),
                            reads=[("SP", sl), "sbmask"], writes=[("SP", sl)])

            def stageB(s):
                d = steps[s]
                hb = d["hl"] * 64
                sl = s % 2
                kb, qt = d["kb"], d["qt"]
                ph.pe(lambda e: e.matmul(pb[sl][:], KT[hb:hb + 64, kb * 128:(kb + 1) * 128],
                                         QT[hb:hb + 64, qt * 512:(qt + 1) * 512], start=True, stop=False),
                      reads=["QT", "KT"], writes=[ps_res(("b", sl))])
                ph.pe(lambda e: e.matmul(pb[sl][:], trineg[:], SPs[sl][:], start=False, stop=True),
                      reads=[("SP", sl), "trineg"], writes=[ps_res(("b", sl))])
                last = (d["t"] == d["nkb"] - 1)
                if not last:
                    ph.pe(lambda e: e.matmul(pd[sl][:], onesneg[:], SPs[sl][:], start=True, stop=True),
                          reads=[("SP", sl), "onesneg"], writes=[ps_res(("d", sl))])
                if d["t"] > 0:
                    ph.dve(lambda e: e.tensor_tensor(out=Bss[sl][:], in0=pb[sl][:], in1=Cc[:], op=ALU.add),
                           reads=[ps_res(("b", sl)), "C"], writes=[("Bs", sl)])
                if not last:
                    if d["t"] == 0:
                        ph.dve(lambda e: e.tensor_copy(out=Cc[:], in_=pd[sl][:]),
                               reads=[ps_res(("d", sl))], writes=["C"])
                    else:
                        ph.dve(lambda e: e.tensor_tensor(out=Cc[:], in0=pd[sl][:], in1=Cc[:], op=ALU.add),
                               reads=[ps_res(("d", sl)), "C"], writes=["C"])

            def stageC(s):
                d = steps[s]
                sl = s % 2
                if d["t"] > 0:
                    ph.act(lambda e: e.activation(out=Ws[sl][:], in_=Bss[sl][:], func=AF.Exp),
                           reads=[("Bs", sl)], writes=[("W", sl)])
                else:
                    ph.act(lambda e: e.activation(out=Ws[sl][:], in_=pb[sl][:], func=AF.Exp),
                           reads=[ps_res(("b", sl))], writes=[("W", sl)])
                if d["diag"] is not None:
                    i = d["diag"]
                    ph.pool(lambda e: e.tensor_tensor(out=Ws[sl][:], in0=Ws[sl][:], in1=sbmask[:, i, :], op=ALU.mult),
                            reads=[("W", sl), "sbmask"], writes=[("W", sl)])

            def stageD(s):
                d = steps[s]
                sl = s % 2
                ob = d["hq"] % 2
                kb, hl, qt = d["kb"], d["hl"], d["qt"]
                last = (d["t"] == d["nkb"] - 1)
                ph.pe(lambda e: e.matmul(po[ob][:], Vt[:, kb, hl * 64:(hl + 1) * 64], Ws[sl][:],
                                         start=(d["t"] == 0), stop=last),
                      reads=[("W", sl), "Vt"], writes=[ps_res(("o", ob))])
                if last:
                    hb = hl * 64
                    ph.dve(lambda e: e.tensor_tensor(out=ys[ob][:], in0=po[ob][:],
                                                     in1=GS[hb:hb + 64, qt * 512:(qt + 1) * 512], op=ALU.mult),
                           reads=[ps_res(("o", ob)), "GS"], writes=[("ys", ob)])
                    r0 = hp * 128 + hl * 64
                    ph.dma("sp", lambda e: e.dma_start(out=yg(0, r0, 64, qt * 512, 512), in_=ys[ob][:]),
                           reads=[("ys", ob)], writes=[("ygT", r0, qt)], ext=[("ygX", 0, 0), ("ygX", 0, 1)])

            for tau in range(n + 3):
                if tau < n:
                    stageA(tau)
                if 0 <= tau - 1 < n:
                    stageB(tau - 1)
                if 0 <= tau - 2 < n:
                    stageC(tau - 2)
                if 0 <= tau - 3 < n:
                    stageD(tau - 3)
            ph.emit()


def phase_moba(nc, sems, st, S, featT, vtok, yg, kmean, ident, onehot_d, mbias_d, rel31_d, nm_d):
    NB = S // 256
    for h in range(4):
        hp, hl = h // 2, h % 2
        ph = Phase(nc, sems)
        with ExitStack() as ls:
            Qa = ls.enter_context(nc.sbuf_tensor(U("mbQa"), [96, S], BF16))
            Ka = ls.enter_context(nc.sbuf_tensor(U("mbKa"), [96, S], BF16))
            GS = ls.enter_context(nc.sbuf_tensor(U("mbGS"), [64, S], BF16))
            Va = ls.enter_context(nc.sbuf_tensor(U("mbVa"), [128, S // 128, 128], BF16))
            mb = ls.enter_context(nc.sbuf_tensor(U("mbias"), [128, 512], F32))
            r31 = ls.enter_context(nc.sbuf_tensor(U("mbr31"), [128, 1], F32))
            NM = ls.enter_context(nc.sbuf_tensor(U("mbNM"), [128, 64], F32))
            kmh = ls.enter_context(nc.sbuf_tensor(U("mbkmh"), [64, 32], F32))
            kh = ls.enter_context(nc.sbuf_tensor(U("mbkh"), [64, 32], BF16))
            khf = ls.enter_context(nc.sbuf_tensor(U("mbkhf"), [64, 32], F32))
            kl = ls.enter_context(nc.sbuf_tensor(U("mbkl"), [64, 32], BF16))
            gms = [ls.enter_context(nc.sbuf_tensor(U("mbgm"), [128, 32], F32)) for i in range(2)]
            mx8 = [ls.enter_context(nc.sbuf_tensor(U("mbmx"), [128, 8], F32)) for i in range(2)]
            thr = [ls.enter_context(nc.sbuf_tensor(U("mbthr"), [128, 1], F32)) for i in range(2)]
            selx = [ls.enter_context(nc.sbuf_tensor(U("mbselx"), [128, 96], BF16)) for i in range(2)]
            Sbs = [ls.enter_context(nc.sbuf_tensor(U("mbSb"), [128, 256], F32)) for i in range(2)]
            Ps = [ls.enter_context(nc.sbuf_tensor(U("mbP"), [128, 256], BF16)) for i in range(3)]
            rec = [ls.enter_context(nc.sbuf_tensor(U("mbrec"), [128, 256], F32)) for i in range(2)]
            ytmp = [ls.enter_context(nc.sbuf_tensor(U("mbyt"), [64, 256], F32)) for i in range(2)]
            ys = [ls.enter_context(nc.sbuf_tensor(U("mbys"), [64, 256], BF16)) for i in range(2)]
            pS = [ls.enter_context(nc.psum_tensor(U("mbpS"), [128, 512], F32)) for i in range(4)]
            pO = [ls.enter_context(nc.psum_tensor(U("mbpO"), [128, 512], F32)) for i in range(2)]
            pG = ls.enter_context(nc.psum_tensor(U("mbpG"), [128, 512], F32))
            pT = ls.enter_context(nc.psum_tensor(U("mbpT"), [128, 1024], BF16))

            ph.dma("sp", lambda e: e.dma_start(out=Qa[0:64, :], in_=featT[R_QB + h * 64:R_QB + (h + 1) * 64, :]), writes=["Qq"])
            ph.dma("act", lambda e: e.dma_start(out=Ka[0:64, :], in_=featT[R_KB + h * 64:R_KB + (h + 1) * 64, :]), writes=["Kk"])
            ph.dma("sp", lambda e: e.dma_start(out=Ka[64:96, :], in_=onehot_d), writes=["Koh"])
            ph.dma("act", lambda e: e.dma_start(out=GS[:], in_=featT[R_GB + h * 64:R_GB + (h + 1) * 64, :]), writes=["GS"])
            ph.pool(lambda e: e.memset(Va[:, :, 64:128], 1.0), writes=["Vones"])
            ph.dma("sp", lambda e: e.dma_start(
                out=Va[:, :, 0:64], in_=vtok[:, C_VB + h * 64:C_VB + (h + 1) * 64].rearrange("(t p) c -> p t c", p=128)),
                writes=["Vv"])
            ph.dma("act", lambda e: e.dma_start(out=mb[:], in_=mbias_d[h]), writes=["mb"])
            ph.dma("act", lambda e: e.dma_start(out=r31[:], in_=rel31_d[h]), writes=["r31"])
            ph.dma("act", lambda e: e.dma_start(out=NM[:], in_=nm_d), writes=["NM"])
            ph.pool(lambda e: e.memset(kmh[:], 0.0), writes=["kmh"])
            ph.dma("sp", lambda e: e.dma_start(out=kmh[:, 0:NB], in_=kmean[hl * 64:(hl + 1) * 64, hp, :]),
                   reads=["kmh"], writes=["kmh"])
            for i in range(2):
                ph.pool(lambda e, i=i: e.memset(selx[i][:], 0.0), writes=[("selx", i)])
            ph.dve(lambda e: e.tensor_copy(out=kh[:], in_=kmh[:]), reads=["kmh"], writes=["kh"])
            ph.dve(lambda e: e.tensor_copy(out=khf[:], in_=kh[:]), reads=["kh"], writes=["khf"])
            ph.dve(lambda e: e.tensor_tensor(out=kl[:], in0=kmh[:], in1=khf[:], op=ALU.subtract),
                   reads=["kmh", "khf"], writes=["kl"])
            for c0 in range(0, S, 2048):
                c1 = min(S, c0 + 2048)
                ph.act(lambda e, c0=c0, c1=c1: e.activation(out=GS[:, c0:c1], in_=GS[:, c0:c1], func=AF.Silu),
                       reads=["GS"], writes=["GS"])

            gi = [0]

            def gating(ob):
                for half in range(2):
                    j = 2 * ob + half
                    i = gi[0] % 2
                    gi[0] += 1
                    qs = slice(j * 128, (j + 1) * 128)
                    pg = pG[:, i * 32:(i + 1) * 32]
                    ph.pe(lambda e, qs=qs, pg=pg: e.matmul(pg, Qa[0:64, qs], kh[:], start=True, stop=False),
                          reads=["Qq", "kh"], writes=[ps_res("mbG")])
                    ph.pe(lambda e, qs=qs, pg=pg: e.matmul(pg, Qa[0:64, qs], kl[:], start=False, stop=True),
                          reads=["Qq", "kl"], writes=[ps_res("mbG")])
                    ph.dve(lambda e, i=i, pg=pg, ob=ob: e.tensor_tensor(out=gms[i][:], in0=pg, in1=NM[:, 32 - ob:64 - ob], op=ALU.add),
                           reads=[ps_res("mbG"), "NM"], writes=[("gm", i)])
                    ph.dve(lambda e, i=i: e.max(out=mx8[i][:], in_=gms[i][:]), reads=[("gm", i)], writes=[("mx8", i)])
                    ph.dve(lambda e, i=i: e.tensor_scalar(out=thr[i][:], in0=mx8[i][:, 2:3], scalar1=-1e29, scalar2=None, op0=ALU.max),
                           reads=[("mx8", i)], writes=[("thr", i)])
                    ph.dve(lambda e, i=i: e.tensor_scalar(out=selx[i][:, 64:96], in0=gms[i][:], scalar1=thr[i][:, 0:1], scalar2=1.0,
                                                          op0=ALU.is_ge, op1=ALU.subtract),
                           reads=[("gm", i), ("thr", i)], writes=[("selx", i)])
                    pt = pT[0:96, i * 128:(i + 1) * 128]
                    ph.pe(lambda e, i=i, pt=pt: e.transpose(out=pt, in_=selx[i][:, 0:96], identity=ident[:]),
                          reads=[("selx", i), "ident"], writes=[ps_res("mbT")])
                    ph.dve(lambda e, i=i, qs=qs: e.tensor_copy(out=Qa[64:96, qs], in_=pT[64:96, i * 128:(i + 1) * 128]),
                           reads=[ps_res("mbT")], writes=[("Qsel", ob, half)])

            steps = []
            for ob in range(NB):
                for kt in range(2 * ob + 2):
                    steps.append(dict(ob=ob, kt=kt, first=(kt == 0), last=(kt == 2 * ob + 1)))
            n = len(steps)

            def stageA(s):
                d = steps[s]
                ob, kt = d["ob"], d["kt"]
                sl = s % 4
                ps = pS[sl]
                own = (kt // 2 == ob)
                K = 64 if own else 96
                q0 = ob * 256 + (128 if (own and kt % 2 == 1) else 0)
                q1 = (ob + 1) * 256
                nq = q1 - q0
                ks = slice(kt * 128, (kt + 1) * 128)
                rd = ["Qq", "Kk", "Koh"] + ([] if own else [("Qsel", ob, 0), ("Qsel", ob, 1)])
                ph.pe(lambda e: e.matmul(ps[:, 0:nq], Ka[0:K, ks], Qa[0:K, q0:q1], start=True, stop=True),
                      reads=rd, writes=[ps_res(("mbS", sl))])
                pi = s % 3
                near = own or (kt == 2 * ob - 1)
                if near:
                    bi = s % 2
                    if own:
                        bt = mb[:, 256:256 + nq]
                    else:
                        bt = mb[:, 0:256]
                    ph.dve(lambda e: e.tensor_tensor(out=Sbs[bi][:, 0:nq], in0=ps[:, 0:nq], in1=bt, op=ALU.add),
                           reads=[ps_res(("mbS", sl)), "mb"], writes=[("Sb", bi)])
                    ph.act(lambda e: e.activation(out=Ps[pi][:, 0:nq], in_=Sbs[bi][:, 0:nq], func=AF.Exp),
                           reads=[("Sb", bi)], writes=[("P", pi)])
                else:
                    ph.act(lambda e: e.activation(out=Ps[pi][:, 0:nq], in_=ps[:, 0:nq], func=AF.Exp, bias=r31[:, 0:1]),
                           reads=[ps_res(("mbS", sl)), "r31"], writes=[("P", pi)])
                d["nq"] = nq
                d["pi"] = pi

            def stageB(s):
                d = steps[s]
                ob, kt, nq, pi = d["ob"], d["kt"], d["nq"], d["pi"]
                oi = ob % 2
                po = pO[oi]
                ph.pe(lambda e: e.matmul(po[:, 256 - nq:256], Va[:, kt, :], Ps[pi][:, 0:nq],
                                         start=d["first"], stop=d["last"]),
                      reads=[("P", pi), "Vv", "Vones"], writes=[ps_res(("mbO", oi))])
                if d["last"]:
                    ph.dve(lambda e: e.reciprocal(out=rec[oi][64:128, :], in_=po[64:128, 0:256]),
                           reads=[ps_res(("mbO", oi))], writes=[("rec", oi)])
                    ph.dve(lambda e: e.tensor_tensor(out=ytmp[oi][:], in0=po[0:64, 0:256], in1=rec[oi][64:128, :], op=ALU.mult),
                           reads=[ps_res(("mbO", oi)), ("rec", oi)], writes=[("yt", oi)])
                    ph.pool(lambda e: e.tensor_tensor(out=ys[oi][:], in0=ytmp[oi][:], in1=GS[:, ob * 256:(ob + 1) * 256], op=ALU.mult),
                            reads=[("yt", oi), "GS"], writes=[("ys", oi)])
                    ph.dma("sp", lambda e: e.dma_start(out=yg(1, h * 64, 64, ob * 256, 256), in_=ys[oi][:]),
                           reads=[("ys", oi)], writes=[("ygT", ob)], ext=[("ygX", 1, 0), ("ygX", 1, 1)])

            gating(0)
            for tau in range(n + 1):
                if tau < n:
                    d = steps[tau]
                    if d["first"] and d["ob"] + 1 < NB:
                        gating(d["ob"] + 1)
                    stageA(tau)
                if tau - 1 >= 0:
                    stageB(tau - 1)
            ph.emit()


def phase_swa(nc, sems, st, S, featT, vtok, yg, sinks_d, swbias_d):
    NBLK = S // 128
    ph = Phase(nc, sems)
    with ExitStack() as ls:
        Q4 = ls.enter_context(nc.sbuf_tensor(U("swQ4"), [64, 4, S], BF16))
        Kc = ls.enter_context(nc.sbuf_tensor(U("swKc"), [64, S], BF16))
        G4 = ls.enter_context(nc.sbuf_tensor(U("swG4"), [64, 4, S], BF16))
        Vc = ls.enter_context(nc.sbuf_tensor(U("swVc"), [128, NBLK, 128], BF16))
        swb = ls.enter_context(nc.sbuf_tensor(U("swb"), [128, 2, 4, 128], F32))
        snk = ls.enter_context(nc.sbuf_tensor(U("swsnk"), [128, 4], F32))
        esk = ls.enter_context(nc.sbuf_tensor(U("swesk"), [128, 4], F32))
        Sbs = [ls.enter_context(nc.sbuf_tensor(U("swSb"), [128, 4, 128], F32)) for i in range(2)]
        Ps = [ls.enter_context(nc.sbuf_tensor(U("swP"), [128, 4, 128], BF16)) for i in range(4)]
        rec = [ls.enter_context(nc.sbuf_tensor(U("swrec"), [128, 4, 128], F32)) for i in range(2)]
        ytmp = [ls.enter_context(nc.sbuf_tensor(U("swyt"), [64, 4, 128], F32)) for i in range(2)]
        ys = [ls.enter_context(nc.sbuf_tensor(U("swys"), [64, 4, 128], BF16)) for i in range(2)]
        pS = [ls.enter_context(nc.psum_tensor(U("swpS"), [128, 4, 128], F32)) for i in range(4)]
        pO = [ls.enter_context(nc.psum_tensor(U("swpO"), [128, 4, 128], F32)) for i in range(2)]

        for hh in range(4):
            ph.dma("sp" if hh % 2 == 0 else "act", lambda e, hh=hh: e.dma_start(
                out=Q4[:, hh, :], in_=featT[R_QC + hh * 64:R_QC + (hh + 1) * 64, :]), writes=[("Q4", hh)])
            ph.dma("act" if hh % 2 == 0 else "sp", lambda e, hh=hh: e.dma_start(
                out=G4[:, hh, :], in_=featT[R_GC + hh * 64:R_GC + (hh + 1) * 64, :]), writes=[("G4", hh)])
        ph.dma("sp", lambda e: e.dma_start(out=Kc[:], in_=featT[R_KC:R_KC + 64, :]), writes=["Kc"])
        ph.pool(lambda e: e.memset(Vc[:, :, 64:128], 1.0), writes=["Vones"])
        ph.dma("sp", lambda e: e.dma_start(
            out=Vc[:, :, 0:64], in_=vtok[:, C_VC:C_VC + 64].rearrange("(t p) c -> p t c", p=128)), writes=["Vv"])
        ph.dma("act", lambda e: e.dma_start(out=swb[:], in_=swbias_d), writes=["swb"])
        ph.dma("act", lambda e: e.dma_start(out=snk[:], in_=sinks_d.partition_broadcast(128)), writes=["snk"])
        ph.act(lambda e: e.activation(out=esk[:], in_=snk[:], func=AF.Exp), reads=["snk"], writes=["esk"])
        for hh in range(4):
            for c0 in range(0, S, 2048):
                c1 = min(S, c0 + 2048)
                ph.act(lambda e, c0=c0, c1=c1, hh=hh: e.activation(out=G4[:, hh, c0:c1], in_=G4[:, hh, c0:c1], func=AF.Silu),
                       reads=[("G4", hh)], writes=[("G4", hh)])
        q4res = [("Q4", hh) for hh in range(4)]
        g4res = [("G4", hh) for hh in range(4)]
        sic = [0]

        def blk(j):
                qs = slice(j * 128, (j + 1) * 128)
                oi = j % 2
                parts = [(j, 0)] + ([(j - 1, 1)] if j > 0 else [])
                pis = []
                for (kb, which) in parts:
                    sl = sic[0] % 4
                    bi = sic[0] % 2
                    sic[0] += 1
                    ks = slice(kb * 128, (kb + 1) * 128)
                    ph.pe(lambda e, sl=sl, ks=ks: e.matmul(pS[sl][:], Kc[:, ks], Q4[:, :, qs], start=True, stop=True),
                          reads=q4res + ["Kc"], writes=[ps_res(("swS", sl))])
                    ph.dve(lambda e, sl=sl, bi=bi, which=which: e.tensor_tensor(out=Sbs[bi][:], in0=pS[sl][:], in1=swb[:, which, :, :], op=ALU.add),
                           reads=[ps_res(("swS", sl)), "swb"], writes=[("Sb", bi)])
                    ph.act(lambda e, sl=sl, bi=bi: e.activation(out=Ps[sl][:], in_=Sbs[bi][:], func=AF.Exp),
                           reads=[("Sb", bi)], writes=[("P", sl)])
                    pis.append((sl, kb))
                for idx, (sl, kb) in enumerate(pis):
                    ph.pe(lambda e, sl=sl, kb=kb, idx=idx: e.matmul(pO[oi][:], Vc[:, kb, :], Ps[sl][:],
                                                                   start=(idx == 0), stop=(idx == len(pis) - 1)),
                          reads=[("P", sl), "Vv", "Vones"], writes=[ps_res(("swO", oi))])
                for hh in range(4):
                    ph.dve(lambda e, hh=hh: e.tensor_scalar(out=rec[oi][64:128, hh, :], in0=pO[oi][64:128, hh, :],
                                                            scalar1=esk[64:128, hh:hh + 1], scalar2=None, op0=ALU.add),
                           reads=[ps_res(("swO", oi)), "esk"], writes=[("rec", oi, hh)])
                ph.dve(lambda e: e.reciprocal(out=rec[oi][64:128, :, :], in_=rec[oi][64:128, :, :]),
                       reads=[("rec", oi, hh) for hh in range(4)], writes=[("rec2", oi)])
                ph.dve(lambda e: e.tensor_tensor(out=ytmp[oi][:], in0=pO[oi][0:64, :, :], in1=rec[oi][64:128, :, :], op=ALU.mult),
                       reads=[ps_res(("swO", oi)), ("rec2", oi)], writes=[("yt", oi)])
                ph.pool(lambda e: e.tensor_tensor(out=ys[oi][:], in0=ytmp[oi][:], in1=G4[:, :, qs], op=ALU.mult),
                        reads=[("yt", oi)] + g4res, writes=[("ys", oi)])
                ph.dma("sp", lambda e: e.dma_start(
                    out=yg(2, 0, 256, j * 128, 128).rearrange("(h d) s -> d h s", d=64), in_=ys[oi][:]),
                    reads=[("ys", oi)], writes=[("ygT", j)], ext=[("ygX", 2, 0), ("ygX", 2, 1)])

        for j in range(NBLK):
            blk(j)
        ph.emit()


def phase_tail(nc, sems, st, T, x, normw, wg, yload, wp, wo, xout, ident, fnormw=None, xext=(), yq="act", ypre=None):
    ph = Phase(nc, sems)
    with ExitStack() as ls:
        wgs = ls.enter_context(nc.sbuf_tensor(U("t_wg"), [128, NCH, 3 * D], BF16))
        wps = ls.enter_context(nc.sbuf_tensor(U("t_wp"), [128, 3, 4, D], BF16))
        wos = ls.enter_context(nc.sbuf_tensor(U("t_wo"), [128, NCH, D], BF16))
        nwb = ls.enter_context(nc.sbuf_tensor(U("t_nwb"), [128, D], F32))
        fnb = ls.enter_context(nc.sbuf_tensor(U("t_fnb"), [128, D], F32)) if fnormw is not None else None
        xts = [ls.enter_context(nc.sbuf_tensor(U("t_xt"), [128, D], F32)) for i in range(2)]
        xrs = [ls.enter_context(nc.sbuf_tensor(U("t_xr"), [128, D], F32)) for i in range(2)]
        sq = ls.enter_context(nc.sbuf_tensor(U("t_sq"), [128, D], BF16))
        sss = [ls.enter_context(nc.sbuf_tensor(U("t_ss"), [128, 1], F32)) for i in range(2)]
        rss = [ls.enter_context(nc.sbuf_tensor(U("t_rs"), [128, 4], F32)) for i in range(2)]
        sss2 = [ls.enter_context(nc.sbuf_tensor(U("t_ss2"), [128, 1], F32)) for i in range(2)]
        rss2 = [ls.enter_context(nc.sbuf_tensor(U("t_rs2"), [128, 4], F32)) for i in range(2)]
        hbs = [ls.enter_context(nc.sbuf_tensor(U("t_hb"), [128, D], BF16)) for i in range(2)]
        hTs = [ls.enter_context(nc.sbuf_tensor(U("t_hT"), [128, NCH, 512], BF16)) for i in range(2)]
        sgs = [ls.enter_context(nc.sbuf_tensor(U("t_sg"), [128, 3, 512], BF16)) for i in range(2)]
        yb = [ls.enter_context(nc.sbuf_tensor(U("t_yb"), [128, 12, 512], BF16)) for i in range(2)]
        mT = [ls.enter_context(nc.sbuf_tensor(U("t_mT"), [128, NCH, 512], BF16)) for i in range(2)]
        tm = [ls.enter_context(nc.sbuf_tensor(U("t_tm"), [128, 512], F32)) for i in range(4)]
        xo = [ls.enter_context(nc.sbuf_tensor(U("t_xo"), [128, D], F32)) for i in range(2)]
        ptr = [ls.enter_context(nc.psum_tensor(U("t_ptr"), [128, D], BF16)) for i in range(2)]
        pmm = [ls.enter_context(nc.psum_tensor(U("t_pmm"), [128, 512], F32)) for i in range(6)]

        for c in range(NCH):
            for hh in range(2):
                c0, c1 = hh * 1536, (hh + 1) * 1536
                ph.dma("pool", lambda e, c=c, c0=c0, c1=c1: e.dma_start(
                    out=wgs[:, c, c0:c1], in_=wg[c * 128:(c + 1) * 128, c0:c1]), writes=[("wg", c, hh)])
            ph.dma("pool", lambda e, c=c: e.dma_start(out=wos[:, c, :], in_=wo[c * 128:(c + 1) * 128, :]),
                   writes=[("wo", c)])
        for m in range(3):
            for ec in range(4):
                ph.dma("pool", lambda e, m=m, ec=ec: e.dma_start(out=wps[:, m, ec, :], in_=wp[m, ec * 128:(ec + 1) * 128, :]),
                       writes=[("wp", m, ec)])
        ph.dma("sp", lambda e: e.dma_start(out=nwb[:], in_=normw.partition_broadcast(128)), writes=["normw"])
        if fnormw is not None:
            ph.dma("sp", lambda e: e.dma_start(out=fnb[:], in_=fnormw.partition_broadcast(128)), writes=["fnormw"])
        wgres = [("wg", c, hh) for c in range(NCH) for hh in range(2)]
        wores = [("wo", c) for c in range(NCH)]
        wpres = [("wp", m, ec) for m in range(3) for ec in range(4)]

        nblk = T // 512
        cnt = dict(tt=0, mm=0, tmi=0, xr=0)
        if ypre is not None:
            ypre(ph)

        def newbank():
            i = cnt["mm"] % 6
            cnt["mm"] += 1
            return pmm[i], ps_res(("tmm", i))

        def block(b):
            hT = hTs[b % 2]
            ybb = yb[b % 2]
            mTb = mT[b % 2]
            ts = slice(b * 512, (b + 1) * 512)
            for m in range(3):
                for r in range(2):
                    ph.dma(yq, lambda e, m=m, r=r: yload(e, m, r, ts, ybb[:, m * 4 + r * 2:m * 4 + r * 2 + 2, :]),
                           reads=[("yloc", m, r)], writes=[("yb", b % 2, m, r)], ext=[("G", m)])
            for ti in range(4):
                sl = cnt["tt"] % 2
                cnt["tt"] += 1
                r0 = b * 512 + ti * 128

                def tile(ti=ti, sl=sl, r0=r0):
                    ph.dma("sp", lambda e: e.dma_start(out=xts[sl][:], in_=x[r0:r0 + 128, :]), writes=[("x", "t", sl)], ext=xext)
                    rmsnorm_tile(ph, nc, xts[sl], hbs[sl], sq, sss[sl], rss[sl], nwb, "t", sl)
                    pt = ptr[sl]
                    for c in range(NCH):
                        ph.pe(lambda e, c=c: e.transpose(out=pt[:, c * 128:(c + 1) * 128], in_=hbs[sl][:, c * 128:(c + 1) * 128],
                                                         identity=ident[:]),
                              reads=[("hb", "t", sl), "ident"], writes=[ps_res(("ttr", sl))])
                    ph.dve(lambda e: e.tensor_copy(out=hT[:, :, ti * 128:(ti + 1) * 128],
                                                   in_=pt[:].rearrange("p (c t) -> p c t", c=NCH)),
                           reads=[ps_res(("ttr", sl))], writes=[("hT", b % 2, ti)])
                tile()
            hres = [("hT", b % 2, ti) for ti in range(4)]
            for og in range(NCH):
                def proj(og=og):
                    sg = sgs[og % 2]
                    for m in range(3):
                        gi = m * 8 + og
                        pm, pres = newbank()
                        for c in range(NCH):
                            ph.pe(lambda e, c=c, gi=gi, pm=pm: e.matmul(pm[:], wgs[:, c, gi * 128:(gi + 1) * 128], hT[:, c, :],
                                                                      start=(c == 0), stop=(c == NCH - 1)),
                                  reads=wgres + hres, writes=[pres])
                        ph.act(lambda e, m=m, pm=pm: e.activation(out=sg[:, m, :], in_=pm[:], func=AF.Sigmoid),
                               reads=[pres], writes=[("sg", og % 2, m)])
                    acc = None
                    for m in range(3):
                        pm, pres = newbank()
                        for ec in range(4):
                            ph.pe(lambda e, ec=ec, m=m, pm=pm: e.matmul(pm[:], wps[:, m, ec, og * 128:(og + 1) * 128],
                                                                      ybb[:, m * 4 + ec, :], start=(ec == 0), stop=(ec == 3)),
                                  reads=wpres + [("yb", b % 2, m, 0), ("yb", b % 2, m, 1)], writes=[pres])
                        ti_ = cnt["tmi"] % 4
                        cnt["tmi"] += 1
                        tcur = tm[ti_]
                        ph.dve(lambda e, m=m, pm=pm, tcur=tcur: e.tensor_tensor(out=tcur[:], in0=pm[:], in1=sg[:, m, :], op=ALU.mult),
                               reads=[pres, ("sg", og % 2, m)], writes=[("tm", ti_)])
                        if acc is None:
                            acc = (tcur, ("tm", ti_))
                        elif m == 1:
                            ph.pool(lambda e, a=acc[0], tcur=tcur: e.tensor_tensor(out=tcur[:], in0=tcur[:], in1=a[:], op=ALU.add),
                                    reads=[("tm", ti_), acc[1]], writes=[("tm", ti_)])
                            acc = (tcur, ("tm", ti_))
                        else:
                            ph.pool(lambda e, a=acc[0], tcur=tcur: e.tensor_tensor(out=mTb[:, og, :], in0=tcur[:], in1=a[:], op=ALU.add),
                                    reads=[("tm", ti_), acc[1]], writes=[("mT", b % 2, og)])
                proj()
            mres = [("mT", b % 2, og) for og in range(NCH)]
            for ti in range(4):
                def outp(ti=ti):
                    xs = cnt["xr"] % 2
                    cnt["xr"] += 1
                    r0 = b * 512 + ti * 128
                    ph.dma("sp", lambda e: e.dma_start(out=xrs[xs][:], in_=x[r0:r0 + 128, :]), writes=[("xr", xs)], ext=xext)
                    for half in range(2):
                        pm, pres = newbank()
                        for og in range(NCH):
                            ph.pe(lambda e, og=og, pm=pm, half=half: e.matmul(pm[:], mTb[:, og, ti * 128:(ti + 1) * 128],
                                                                            wos[:, og, half * 512:(half + 1) * 512],
                                                                            start=(og == 0), stop=(og == NCH - 1)),
                                  reads=wores + mres, writes=[pres])
                        ph.dve(lambda e, pm=pm, half=half: e.tensor_tensor(out=xo[xs][:, half * 512:(half + 1) * 512], in0=pm[:],
                                                                          in1=xrs[xs][:, half * 512:(half + 1) * 512], op=ALU.add),
                               reads=[pres, ("xr", xs)], writes=[("xo", xs, half)])
                    if fnormw is None:
                        ph.dma("act", lambda e: e.dma_start(out=xout[r0:r0 + 128, :], in_=xo[xs][:]),
                               reads=[("xo", xs, 0), ("xo", xs, 1)], writes=[("xout", r0)], ext=xext)
                    else:
                        ph.act(lambda e: e.activation(out=sq[:], in_=xo[xs][:], func=AF.Square, accum_out=sss2[xs][:, 0:1]),
                               reads=[("xo", xs, 0), ("xo", xs, 1)], writes=[("sq", "t"), ("ss2", xs)])
                        ph.dve(lambda e: e.tensor_scalar(out=rss2[xs][:, 0:1], in0=sss2[xs][:, 0:1], scalar1=1.0 / D, scalar2=EPS,
                                                         op0=ALU.mult, op1=ALU.add), reads=[("ss2", xs)], writes=[("rsa", xs)])
                        ph.act(lambda e: e.activation(out=rss2[xs][:, 1:2], in_=rss2[xs][:, 0:1], func=AF.Sqrt),
                               reads=[("rsa", xs)], writes=[("rsb", xs)])
                        ph.dve(lambda e: e.reciprocal(out=rss2[xs][:, 2:3], in_=rss2[xs][:, 1:2]), reads=[("rsb", xs)], writes=[("rsc", xs)])
                        ph.dve(lambda e: e.scalar_tensor_tensor(out=xo[xs][:], in0=xo[xs][:], scalar=rss2[xs][:, 2:3], in1=fnb[:],
                                                                op0=ALU.mult, op1=ALU.mult),
                               reads=[("xo", xs, 0), ("xo", xs, 1), ("rsc", xs), "fnormw"], writes=[("xf", xs)])
                        ph.dma("act", lambda e: e.dma_start(out=xout[r0:r0 + 128, :], in_=xo[xs][:]),
                               reads=[("xf", xs)], writes=[("xout", r0)])
                outp()

        for b in range(nblk):
            block(b)
        ph.emit()

bf = ml_dtypes.bfloat16
NEGC = -30000.0


def rel_bucket_np(dist):
    n = np.maximum(dist, 0)
    nf = np.maximum(n, 1).astype(np.float32)
    large = 16 + (np.log(nf / np.float32(16)) / np.float32(np.log(8.0)) * np.float32(16)).astype(np.int32)
    large = np.minimum(large, 31)
    return np.where(n < 16, n, large)


def toeplitz_tiles(rel_col):
    p = np.arange(128)[:, None]
    f = np.arange(128)[None, :]
    d0 = f - p
    diag = np.where(d0 >= 0, rel_col[rel_bucket_np(d0)], np.float32(NEGC)).astype(np.float32)
    d1 = 128 + f - p
    off1 = rel_col[rel_bucket_np(d1)].astype(np.float32)
    return diag, off1


def make_consts(S):
    c = {}
    c["c_ident"] = np.eye(128, dtype=np.float32).astype(bf)
    k = np.arange(128)[:, None]
    m = np.arange(128)[None, :]
    c["c_tri"] = np.where(k >= m, -1.0, 0.0).astype(bf)
    c["c_ones"] = (-np.ones((128, 128), np.float32)).astype(bf)
    mask = np.zeros((128, 4, 512), np.float32)
    for i in range(4):
        mask[:, i, :] = ((i * 128 + np.arange(128))[:, None] < np.arange(512)[None, :])
    c["c_mask"] = mask.astype(bf)
    oh = np.zeros((32, S), np.float32)
    for n in range(S // 256):
        oh[n, n * 256:(n + 1) * 256] = 29952.0
    c["c_onehot"] = oh.astype(bf)
    nm = np.zeros((128, 64), np.float32)
    nm[:, 32:] = -1e30
    c["c_nm"] = nm
    return c


def make_bias(rel_bias, hh):
    mb = np.zeros((4, 128, 512), np.float32)
    r31 = np.zeros((4, 128, 1), np.float32)
    sw = np.zeros((128, 2, 4, 128), np.float32)
    p = np.arange(128)[:, None]
    f = np.arange(128)[None, :]
    for h in range(4):
        col = rel_bias[:, 4 * hh + h]
        diag, off1 = toeplitz_tiles(col)
        mb[h, :, 0:128] = off1
        mb[h, :, 128:256] = col[31]
        mb[h, :, 256:384] = diag
        mb[h, :, 384:512] = off1
        r31[h, :, 0] = col[31]
        cols = rel_bias[:, 8 + 4 * hh + h]
        diag, off1 = toeplitz_tiles(cols)
        sw[:, 0, h, :] = diag
        sw[:, 1, h, :] = np.where(f < p, off1, np.float32(NEGC))
    return mb, r31, sw

SEQ = 8192
BATCH = 4
DEPTH = 2
THALF = SEQ // 2
O_QA, O_KA, O_VA, O_GA = 0, 512, 1024, 1536
O_QB, O_KB, O_VB, O_GB = 2048, 2560, 3072, 3584
O_QC, O_KC, O_VC, O_GC = 4096, 4608, 4736, 4864
O_MA = 5376
PAIRS = [[0, 1], [2, 3], [4, 5], [6, 7]]


def build_fused(S, depth=DEPTH, stop_after=None):
    T = S // 2
    NW = NFEAT + NV
    nc = bass.Bass("TRN2", target_bir_lowering=False)
    xfull = nc.dram_tensor("xfull", [S, D], F32, kind="ExternalInput").ap()
    xown = nc.dram_tensor("xown", [T, D], F32, kind="ExternalInput").ap()
    normw = nc.dram_tensor("normw", [DEPTH, D], F32, kind="ExternalInput").ap()
    fnw = nc.dram_tensor("fnw", [D], F32, kind="ExternalInput").ap()
    w = nc.dram_tensor("w", [DEPTH, D, NW], F32, kind="ExternalInput").ap()
    wg = nc.dram_tensor("wg", [DEPTH, D, 3 * D], F32, kind="ExternalInput").ap()
    wp = nc.dram_tensor("wp", [DEPTH, 3, 512, D], F32, kind="ExternalInput").ap()
    wo = nc.dram_tensor("wo", [DEPTH, D, D], F32, kind="ExternalInput").ap()
    sinks = nc.dram_tensor("sinks", [DEPTH, 4], F32, kind="ExternalInput").ap()
    c_ident = nc.dram_tensor("c_ident", [128, 128], BF16, kind="ExternalInput").ap()
    c_tri = nc.dram_tensor("c_tri", [128, 128], BF16, kind="ExternalInput").ap()
    c_ones = nc.dram_tensor("c_ones", [128, 128], BF16, kind="ExternalInput").ap()
    c_mask = nc.dram_tensor("c_mask", [128, 4, 512], BF16, kind="ExternalInput").ap()
    c_onehot = nc.dram_tensor("c_onehot", [32, S], BF16, kind="ExternalInput").ap()
    c_nm = nc.dram_tensor("c_nm", [128, 64], F32, kind="ExternalInput").ap()
    mbias = nc.dram_tensor("mbias", [4, 128, 512], F32, kind="ExternalInput").ap()
    rel31 = nc.dram_tensor("rel31", [4, 128, 1], F32, kind="ExternalInput").ap()
    swbias = nc.dram_tensor("swbias", [128, 2, 4, 128], F32, kind="ExternalInput").ap()
    out = nc.dram_tensor("out", [T, D], F32, kind="ExternalOutput").ap()
    featT = nc.dram_tensor("featT", [NFEAT, S], BF16).ap()
    vtok = nc.dram_tensor("vtok", [S, NV], BF16).ap()
    ygX = [nc.dram_tensor("ygX%d" % m, [2 * 256, T], BF16).ap() for m in range(3)]
    G = [nc.dram_tensor("G%d" % m, [4 * 256, T], BF16).ap() for m in range(3)]
    xnh = nc.dram_tensor("xnh", [T, D], F32).ap()
    XCH = min(512, T)
    NXC = T // XCH
    xg = nc.dram_tensor("xg", [NXC * 2 * XCH, D], F32).ap()

    def yg(m, r0, n, t0, nt):
        th = t0 // T
        return ygX[m][th * 256 + r0:th * 256 + r0 + n, t0 - th * T:t0 - th * T + nt]

    yloc = [nc.dram_tensor("yloc%d" % m, [2 * 256, T], BF16).ap() for m in range(3)]

    hv = {}

    def ypre(ph):
        hv.clear()
        for m in range(3):
            for r in range(2):
                def cp(e, m=m, r=r):
                    if not hv:
                        hv[0] = e.snap((e.partition_id() % 2) * 2)
                        hv[1] = e.snap(hv[0] + 1)
                    g = G[m].rearrange("(k q) s -> k q s", q=256)
                    return e.dma_start(out=yloc[m][r * 256:(r + 1) * 256, :], in_=g[bass.ds(hv[r], 1), :, :])
                ph.dma("pool", cp, writes=[("yloc", m, r)], ext=[("G", m, 0), ("G", m, 1)])

    def yload(e, m, r, ts, dst):
        return e.dma_start(out=dst, in_=yloc[m][r * 256:(r + 1) * 256, ts].rearrange("(c p) s -> p c s", p=128))

    def ag(src, dst):
        return lambda e: e.collective_compute("AllGather", ALU.bypass, replica_groups=PAIRS, ins=[src], outs=[dst])

    with ExitStack() as st:
        sems = Sems(nc, st)
        ident = st.enter_context(nc.sbuf_tensor("ident", [128, 128], BF16))
        trineg = st.enter_context(nc.sbuf_tensor("trineg", [128, 128], BF16))
        onesneg = st.enter_context(nc.sbuf_tensor("onesneg", [128, 128], BF16))
        sbmask = st.enter_context(nc.sbuf_tensor("sbmask", [128, 4, 512], BF16))
        kmean = st.enter_context(nc.sbuf_tensor("kmean", [128, 2, S // 256], F32))
        ph = Phase(nc, sems)
        ph.dma("sp", lambda e: e.dma_start(out=ident[:], in_=c_ident), writes=["ident"])
        ph.dma("sp", lambda e: e.dma_start(out=trineg[:], in_=c_tri), writes=["trineg"])
        ph.dma("sp", lambda e: e.dma_start(out=onesneg[:], in_=c_ones), writes=["onesneg"])
        ph.dma("sp", lambda e: e.dma_start(out=sbmask[:], in_=c_mask), writes=["sbmask"])
        ph.emit()
        for layer in range(depth):
            last = (layer == depth - 1)
            if layer > 0:
                fresh_counters(sems)
            if layer == 0:
                xsrc = lambda r0: (xfull[r0:r0 + 128, :], ())
            else:
                def xsrc(r0):
                    r, k, i = r0 // T, (r0 % T) // XCH, r0 % XCH
                    row = (k * 2 + r) * XCH + i
                    return xg[row:row + 128, :], [("xg", k)]
            phase_front(nc, sems, st, S, xsrc, normw[layer], w[layer], featT, vtok, kmean, ident)
            phase_swa(nc, sems, st, S, featT, vtok, yg, sinks[layer], swbias)
            for th in range(2):
                sems.pending.append((ag(ygX[2][th * 256:(th + 1) * 256, :], G[2][th * 512:(th + 1) * 512, :]),
                                     [("ygX", 2, th), ("G", 2, th)]))
            phase_moba(nc, sems, st, S, featT, vtok, yg, kmean, ident, c_onehot, mbias, rel31, c_nm)
            for th in range(2):
                sems.pending.append((ag(ygX[1][th * 256:(th + 1) * 256, :], G[1][th * 512:(th + 1) * 512, :]),
                                     [("ygX", 1, th), ("G", 1, th)]))
            phase_sb(nc, sems, st, S, featT, vtok, yg, trineg, onesneg, sbmask)
            for th in range(2):
                sems.pending.append((ag(ygX[0][th * 256:(th + 1) * 256, :], G[0][th * 512:(th + 1) * 512, :]),
                                     [("ygX", 0, th), ("G", 0, th)]))
            phase_tail(nc, sems, st, T, xown if layer == 0 else xnh, normw[layer], wg[layer], yload, wp[layer], wo[layer],
                       out if last else xnh, ident, fnormw=(fnw if last else None), xext=("xnh",), ypre=ypre)
            if not last:
                for k in range(NXC):
                    sems.pending.append((ag(xnh[k * XCH:(k + 1) * XCH, :], xg[k * 2 * XCH:(k + 1) * 2 * XCH, :]), [("xg", k)]))
            if stop_after == "agx" and layer == 0:
                ph = Phase(nc, sems)
                ph.dma("sp", lambda e: e.dma_start(out=out[0:XCH, :], in_=xg[0:XCH, :]), writes=["o"], ext=[("xg", 0)])
                ph.emit()
                break
            if stop_after == "front1" and layer == 1:
                pass
        assert not sems.pending
        assert max(sems.cnt_val.values()) < 32000, sems.cnt_val
    return nc


def core_w_cols(hh):
    a = lambda o, n: list(range(o + hh * n, o + (hh + 1) * n))
    feat = (a(O_QA, 256) + a(O_KA, 256) + a(O_GA, 256) + a(O_QB, 256) + a(O_KB, 256) + a(O_GB, 256)
            + a(O_QC, 256) + a(O_GC, 256) + a(O_KC, 64))
    val = a(O_VA, 256) + a(O_VB, 256) + a(O_VC, 64)
    return np.array(feat + val)


def kernel(x, norm_w, w_in, w_proj_a, w_proj_b, w_proj_c, w_out, sinks, rel_bias, final_norm_w):
    x = np.ascontiguousarray(np.asarray(x, dtype=np.float32))
    norm_w = np.ascontiguousarray(np.asarray(norm_w, np.float32))
    w_in = np.asarray(w_in, np.float32)
    wp = np.ascontiguousarray(np.stack([np.asarray(w_proj_a, np.float32), np.asarray(w_proj_b, np.float32),
                                        np.asarray(w_proj_c, np.float32)], 1))
    w_out = np.ascontiguousarray(np.asarray(w_out, np.float32))
    sinks = np.asarray(sinks, np.float32)
    rel_bias = np.asarray(rel_bias, np.float32)
    final_norm_w = np.ascontiguousarray(np.asarray(final_norm_w, np.float32))
    S = SEQ
    cs = make_consts(S)
    bias = [make_bias(rel_bias, hh) for hh in range(2)]
    wcore = [np.ascontiguousarray(w_in[:, :, core_w_cols(hh)]) for hh in range(2)]
    wg = np.ascontiguousarray(w_in[:, :, O_MA:O_MA + 3 * D])
    nc = build_fused(S)
    cores = list(range(8))
    in_maps = []
    for c in cores:
        b, hh = c // 2, c % 2
        mb, r31, sw = bias[hh]
        in_maps.append({
            "xfull": x[b], "xown": np.ascontiguousarray(x[b, hh * THALF:(hh + 1) * THALF]),
            "normw": norm_w, "fnw": final_norm_w, "w": wcore[hh], "wg": wg, "wp": wp, "wo": w_out,
            "sinks": np.ascontiguousarray(sinks[:, 4 * hh:4 * hh + 4]),
            "c_ident": cs["c_ident"], "c_tri": cs["c_tri"], "c_ones": cs["c_ones"], "c_mask": cs["c_mask"],
            "c_onehot": cs["c_onehot"], "c_nm": cs["c_nm"], "mbias": mb, "rel31": r31, "swbias": sw,
        })
    res = run_bass_kernel_spmd(nc, in_maps, core_ids=cores)
    outs = [np.asarray(r["out"]) for r in res.results]
    return np.stack([np.concatenate([outs[2 * b], outs[2 * b + 1]], 0) for b in range(BATCH)], 0).astype(np.float32)
```
